# Optimizing a Trainium2 kernel written in Bass

```python
import jax, jax.numpy as jnp
from jax import lax
import numpy as np

D_MODEL = 1024
BATCH = 16
SEQ = 2048
DEPTH = 2
DEC_BATCH = 128
DEC_SEQ = 1
PAST_LEN = 16384
PAGE_SIZE = 128

LRU_WIDTH = D_MODEL
LRU_BLOCKS = 16
LRU_BLOCK = LRU_WIDTH // LRU_BLOCKS
CONV_W = 4
LRU_C = 8.0
N_HEADS = 16
N_KV = 4
HEAD_DIM = D_MODEL // N_HEADS
GROUP = N_HEADS // N_KV
WINDOW = 128
Q_W = N_HEADS * HEAD_DIM
KV_W = N_KV * HEAD_DIM
N_MEM = 256
N_XHEADS = 4
XHEAD_DIM = D_MODEL // N_XHEADS
XW = N_XHEADS * XHEAD_DIM
N_BRANCH = 3
D_FF = -(-8 * D_MODEL // (3 * 256)) * 256
LN_EPS = 1e-5
ALPHA = (2 * DEPTH) ** 0.25
BETA = (8 * DEPTH) ** -0.25
COL_WIDTHS = (LRU_WIDTH, LRU_WIDTH, Q_W, KV_W, KV_W, XW, N_BRANCH * D_MODEL)
SPLIT_IDX = [int(c) for c in np.cumsum(COL_WIDTHS)[:-1]]
IN_COLS = int(sum(COL_WIDTHS))

kernel_name = "hybrid_rglru_swa_memxattn_decode_step"


def layer_norm(x, g, b):
    xf = x.astype(jnp.float32)
    mu = xf.mean(-1, keepdims=True)
    var = jnp.square(xf - mu).mean(-1, keepdims=True)
    y = (xf - mu) * lax.rsqrt(var + LN_EPS) * g.astype(jnp.float32) + b.astype(jnp.float32)
    return y.astype(x.dtype)


def causal_conv(xb, buf, w, b):
    T = xb.shape[1]
    xc = jnp.concatenate([buf, xb], axis=1)
    y = b + sum(xc[:, k:k + T] * w[k] for k in range(CONV_W))
    return y, xc[:, -(CONV_W - 1):]


def rg_lru(xc, h0, wa, ba, wx, bx, lam):
    B, T, C = xc.shape
    f32 = jnp.float32
    xf = xc.astype(f32)
    xblk = xf.reshape(B, T, LRU_BLOCKS, LRU_BLOCK)
    r = jax.nn.sigmoid(jnp.einsum('btni,nij->btnj', xblk, wa.astype(f32)).reshape(B, T, C) + ba.astype(f32))
    i = jax.nn.sigmoid(jnp.einsum('btni,nij->btnj', xblk, wx.astype(f32)).reshape(B, T, C) + bx.astype(f32))
    log_a = -LRU_C * r * jax.nn.softplus(-lam.astype(f32))
    a = jnp.exp(log_a)
    u = jnp.sqrt(-jnp.expm1(2.0 * log_a)) * (i * xf)

    def step(h, au):
        a_t, u_t = au
        h = a_t * h + u_t
        return h, h

    hT, hs = lax.scan(step, h0.astype(f32), (jnp.swapaxes(a, 0, 1), jnp.swapaxes(u, 0, 1)))
    return jnp.swapaxes(hs, 0, 1).astype(xc.dtype), hT.astype(h0.dtype)


def sink_softmax(s, mask, sinks):
    sk = sinks.astype(jnp.float32).reshape(N_KV, GROUP)[..., None, None]
    s = jnp.where(mask, s, -1e30)
    mx = jnp.maximum(s.max(-1, keepdims=True), sk)
    e = jnp.exp(s - mx)
    return e / (e.sum(-1, keepdims=True) + jnp.exp(sk - mx))


def band_mask(nq, nk_prefix):
    qi = jnp.arange(nq)[:, None]
    m = jnp.arange(nk_prefix + nq)[None, :]
    return (m <= qi + nk_prefix) & (m > qi + nk_prefix - WINDOW)


def swa_prompt(q, k, v, sinks):
    B, T = q.shape[:2]
    nb = T // WINDOW
    scale = HEAD_DIM ** -0.5
    qb = q.reshape(B, nb, WINDOW, N_KV, GROUP, HEAD_DIM)
    kb = k.reshape(B, nb, WINDOW, N_KV, HEAD_DIM)
    vb = v.reshape(B, nb, WINDOW, N_KV, HEAD_DIM)
    pad = ((0, 0), (1, 0), (0, 0), (0, 0), (0, 0))
    kband = jnp.concatenate([jnp.pad(kb[:, :-1], pad), kb], axis=2)
    vband = jnp.concatenate([jnp.pad(vb[:, :-1], pad), vb], axis=2)
    s = jnp.einsum('bnqkgd,bnmkd->bnkgqm', qb, kband).astype(jnp.float32) * scale
    band = band_mask(WINDOW, WINDOW)
    m = jnp.arange(2 * WINDOW)[None, :]
    real = (jnp.arange(nb)[:, None, None] > 0) | (m >= WINDOW)[None]
    mask = (band[None] & real)[None, :, None, None]
    p = sink_softmax(s, mask, sinks).astype(v.dtype)
    o = jnp.einsum('bnkgqm,bnmkd->bnqkgd', p, vband)
    return o.reshape(B, T, Q_W)


def swa_decode(q, k, v, win_k, win_v, sinks):
    B, T = q.shape[:2]
    scale = HEAD_DIM ** -0.5
    kall = jnp.concatenate([win_k, k], axis=1)
    vall = jnp.concatenate([win_v, v], axis=1)
    qg = q.reshape(B, T, N_KV, GROUP, HEAD_DIM)
    s = jnp.einsum('btkgd,bmkd->bkgtm', qg, kall).astype(jnp.float32) * scale
    p = sink_softmax(s, band_mask(T, WINDOW), sinks).astype(v.dtype)
    o = jnp.einsum('bkgtm,bmkd->btkgd', p, vall)
    return o.reshape(B, T, Q_W), kall[:, -WINDOW:], vall[:, -WINDOW:]


def mem_attn(q, mk, mv):
    s = jnp.einsum('bthd,bmhd->bhtm', q, mk).astype(jnp.float32) * (XHEAD_DIM ** -0.5)
    p = jax.nn.softmax(s, axis=-1).astype(mv.dtype)
    B, T = q.shape[:2]
    return jnp.einsum('bhtm,bmhd->bthd', p, mv).reshape(B, T, XW)


def decoder_layer(x, conv_buf, h0, win_k, win_v, mem_k, mem_v, is_prompt, lw):
    (w_in, b_gates, conv_w, conv_b, lru_wa, lru_ba, lru_wx, lru_bx, lru_lambda, sinks,
     w_branch, w_out, ln1_g, ln1_b, w_gate_up, w_down, ln2_g, ln2_b) = lw
    B, T, _ = x.shape
    z = x @ w_in
    xr, yr, q, k, v, xq, g = jnp.split(z, SPLIT_IDX, axis=-1)
    xc, conv_new = causal_conv(xr, conv_buf, conv_w, conv_b)
    hs, h_new = rg_lru(xc, h0, lru_wa, lru_ba, lru_wx, lru_bx, lru_lambda)
    a_out = hs * jax.nn.gelu(yr, approximate=True)
    q = q.reshape(B, T, N_HEADS, HEAD_DIM)
    k = k.reshape(B, T, N_KV, HEAD_DIM)
    v = v.reshape(B, T, N_KV, HEAD_DIM)
    if is_prompt:
        s_out = swa_prompt(q, k, v, sinks)
        wk_new, wv_new = k[:, -WINDOW:], v[:, -WINDOW:]
    else:
        s_out, wk_new, wv_new = swa_decode(q, k, v, win_k, win_v, sinks)
    c_out = mem_attn(xq.reshape(B, T, N_XHEADS, XHEAD_DIM), mem_k, mem_v)
    gates = jax.nn.sigmoid(g.reshape(B, T, N_BRANCH, D_MODEL) + b_gates)
    merged = sum(gates[:, :, n] * (br @ w_branch[n]) for n, br in enumerate((a_out, s_out, c_out)))
    h = layer_norm(ALPHA * x + merged @ w_out, ln1_g, ln1_b)
    gt, up = jnp.split(h @ w_gate_up, 2, axis=-1)
    h = layer_norm(ALPHA * h + (jax.nn.silu(gt) * up) @ w_down, ln2_g, ln2_b)
    return h, conv_new, h_new, wk_new, wv_new


def setup_inputs(seed: int = 0) -> dict:
    key = jax.random.key(seed)
    ks = iter(jax.random.split(key, 40))
    nrm = lambda shape, s=1.0: jax.random.normal(next(ks), shape, jnp.float32) * s
    L = DEPTH
    a_init = jax.random.uniform(next(ks), (L, LRU_WIDTH), jnp.float32, 0.9, 0.999)
    return {
        "x_prompt": nrm((BATCH, SEQ, D_MODEL)),
        "x_sample": nrm((DEC_BATCH, DEC_SEQ, D_MODEL)),
        "state_conv": nrm((L, DEC_BATCH, CONV_W - 1, LRU_WIDTH)),
        "state_h": nrm((L, DEC_BATCH, LRU_WIDTH)),
        "cache_win_k": nrm((L, DEC_BATCH, WINDOW, N_KV, HEAD_DIM)),
        "cache_win_v": nrm((L, DEC_BATCH, WINDOW, N_KV, HEAD_DIM)),
        "cache_mem_k": nrm((L, DEC_BATCH, N_MEM, N_XHEADS, XHEAD_DIM)),
        "cache_mem_v": nrm((L, DEC_BATCH, N_MEM, N_XHEADS, XHEAD_DIM)),
        "mem_prompt": nrm((BATCH, N_MEM, D_MODEL)),
        "w_mem_kv": nrm((L, D_MODEL, 2 * XW), D_MODEL ** -0.5),
        "w_in": nrm((L, D_MODEL, IN_COLS), D_MODEL ** -0.5),
        "b_gates": nrm((L, N_BRANCH, D_MODEL), 0.02),
        "conv_w": nrm((L, CONV_W, LRU_WIDTH), CONV_W ** -0.5),
        "conv_b": nrm((L, LRU_WIDTH), 0.02),
        "lru_wa": nrm((L, LRU_BLOCKS, LRU_BLOCK, LRU_BLOCK), LRU_BLOCK ** -0.5),
        "lru_ba": nrm((L, LRU_WIDTH), 0.02),
        "lru_wx": nrm((L, LRU_BLOCKS, LRU_BLOCK, LRU_BLOCK), LRU_BLOCK ** -0.5),
        "lru_bx": nrm((L, LRU_WIDTH), 0.02),
        "lru_lambda": jnp.log(a_init) - jnp.log1p(-a_init),
        "sinks": nrm((L, N_HEADS), 0.5),
        "w_branch": nrm((L, N_BRANCH, D_MODEL, D_MODEL), D_MODEL ** -0.5),
        "w_out": nrm((L, D_MODEL, D_MODEL), BETA * D_MODEL ** -0.5),
        "ln1_g": 1.0 + nrm((L, D_MODEL), 0.02),
        "ln1_b": nrm((L, D_MODEL), 0.02),
        "w_gate_up": nrm((L, D_MODEL, 2 * D_FF), D_MODEL ** -0.5),
        "w_down": nrm((L, D_FF, D_MODEL), BETA * D_FF ** -0.5),
        "ln2_g": 1.0 + nrm((L, D_MODEL), 0.02),
        "ln2_b": nrm((L, D_MODEL), 0.02),
    }


def reference(x_prompt, x_sample, state_conv, state_h, cache_win_k, cache_win_v, cache_mem_k, cache_mem_v,
              mem_prompt, w_mem_kv, w_in, b_gates, conv_w, conv_b, lru_wa, lru_ba, lru_wx, lru_bx, lru_lambda,
              sinks, w_branch, w_out, ln1_g, ln1_b, w_gate_up, w_down, ln2_g, ln2_b):
    xp, xs = x_prompt, x_sample
    Bp = xp.shape[0]
    p_conv, p_h, p_wk, p_wv, p_mk, p_mv = [], [], [], [], [], []
    s_conv, s_h, s_wk, s_wv = [], [], [], []
    zero_conv = jnp.zeros((Bp, CONV_W - 1, LRU_WIDTH), xp.dtype)
    zero_h = jnp.zeros((Bp, LRU_WIDTH), xp.dtype)
    for l in range(DEPTH):
        lw = (w_in[l], b_gates[l], conv_w[l], conv_b[l], lru_wa[l], lru_ba[l], lru_wx[l], lru_bx[l],
              lru_lambda[l], sinks[l], w_branch[l], w_out[l], ln1_g[l], ln1_b[l], w_gate_up[l], w_down[l],
              ln2_g[l], ln2_b[l])
        mkv = (mem_prompt @ w_mem_kv[l]).reshape(Bp, N_MEM, 2, N_XHEADS, XHEAD_DIM)
        mk, mv = mkv[:, :, 0], mkv[:, :, 1]
        xp, c_new, h_new, wk, wv = decoder_layer(xp, zero_conv, zero_h, None, None, mk, mv, True, lw)
        p_conv.append(c_new); p_h.append(h_new); p_wk.append(wk); p_wv.append(wv)
        p_mk.append(mk); p_mv.append(mv)
        xs, c_new, h_new, wk, wv = decoder_layer(xs, state_conv[l], state_h[l], cache_win_k[l], cache_win_v[l],
                                                 cache_mem_k[l], cache_mem_v[l], False, lw)
        s_conv.append(c_new); s_h.append(h_new); s_wk.append(wk); s_wv.append(wv)
    return (xp, xs,
            jnp.stack(p_conv), jnp.stack(p_h), jnp.stack(p_wk), jnp.stack(p_wv), jnp.stack(p_mk), jnp.stack(p_mv),
            jnp.stack(s_conv), jnp.stack(s_h), jnp.stack(s_wk), jnp.stack(s_wv))
```

```python
import numpy as np
from contextlib import ExitStack
import concourse.bass as bass
import concourse.mybir as mybir

F32 = mybir.dt.float32
BF16 = mybir.dt.bfloat16
I32 = mybir.dt.int32
AF = mybir.ActivationFunctionType
ALU = mybir.AluOpType
AX = mybir.AxisListType

_DT_SIZE = {F32: 4, BF16: 2, I32: 4}


def _box(ap):
    t = ap.tensor
    name = t.name
    esz = _DT_SIZE[ap.dtype]
    pat = ap.ap
    off = int(ap.offset)
    space = str(ap.space)
    if "DRAM" in space.upper() or "HBM" in space.upper() or "Dram" in space:
        ext = 0
        for st, cnt in pat:
            ext += abs(st) * (cnt - 1)
        return (name, 0, 1, off * esz, (off + ext + 1) * esz)
    shp = list(t.shape)
    pstride = 1
    for s in shp[1:]:
        pstride *= s
    p0 = off // pstride
    f0 = off % pstride
    st0, cnt0 = pat[0]
    assert st0 == pstride or cnt0 == 1, (name, pat, pstride)
    ext = 0
    for st, cnt in pat[1:]:
        ext += abs(st) * (cnt - 1)
    lo, hi = f0 * esz, (f0 + ext + 1) * esz
    if "PSUM" in space.upper():
        lo = lo // 2048 * 2048
        hi = (hi + 2047) // 2048 * 2048
    return (name, p0, p0 + cnt0, lo, hi)


class _Op:
    __slots__ = ("eng", "fn", "waits", "inc", "dma_sem", "dma_val", "idx", "ms")

    def __init__(self, eng, fn):
        self.eng = eng
        self.fn = fn
        self.waits = []
        self.inc = False
        self.dma_sem = None
        self.dma_val = 0
        self.ms = 0


ENGS = ("pe", "act", "dve", "pool", "sp")
N_DMA_SEMS = {"sp": 12, "pool": 12, "act": 4}


class Prog:
    def __init__(self, nc):
        self.nc = nc
        self.ops = {e: [] for e in ENGS}
        self.recs = {}
        self.dma_cnt = {q: [0] * n for q, n in N_DMA_SEMS.items()}
        self.dma_rr = {q: 0 for q in N_DMA_SEMS}
        self.dma_last = {q: [None] * n for q, n in N_DMA_SEMS.items()}

    def _deps(self, op_eng, is_dma, reads, writes, ev):
        deps = []
        for kind, aps in (("r", reads), ("w", writes)):
            for ap in aps:
                name, p0, p1, b0, b1 = _box(ap)
                lst = self.recs.setdefault(name, [])
                keep = []
                for rec in lst:
                    rp0, rp1, rb0, rb1, rk, rev = rec
                    ov = not (rp1 <= p0 or p1 <= rp0 or rb1 <= b0 or b1 <= rb0)
                    rar_psum = (name == "PS" and kind == "r" and rk == "r" and rev[1] != op_eng
                                and not (rb1 // 2048 * 2048 + (2048 if rb1 % 2048 else 0) <= b0 // 2048 * 2048
                                         or b1 // 2048 * 2048 + (2048 if b1 % 2048 else 0) <= rb0 // 2048 * 2048))
                    if rar_psum and rev != ev:
                        deps.append(rev)
                    if ov and rev != ev and not (kind == "r" and rk == "r"):
                        same_eng = (rev[0] == "c" and not is_dma and rev[1] == op_eng)
                        if not (same_eng and op_eng == "pe"):
                            deps.append(rev)
                    drop = False
                    if kind == "w" and p0 <= rp0 and rp1 <= p1 and b0 <= rb0 and rb1 <= b1:
                        drop = True
                    if (kind == "r" and rk == "r" and rev[0] == "c" and ev[0] == "c" and rev[1] == ev[1]
                            and (rp0, rp1, rb0, rb1) == (p0, p1, b0, b1)):
                        drop = True
                    if not drop:
                        keep.append(rec)
                keep.append([p0, p1, b0, b1, kind, ev])
                self.recs[name] = keep
        return deps

    def add(self, eng, fn, reads=(), writes=()):
        op = _Op(eng, fn)
        op.idx = len(self.ops[eng])
        ev = ("c", eng, op.idx)
        op.waits = self._deps(eng, False, [a for a in reads if a is not None and not isinstance(a, (int, float))],
                              list(writes), ev)
        self.ops[eng].append(op)
        return op

    def dma(self, q, out, in_, **kw):
        op = _Op(q, lambda e: e.dma_start(out=out, in_=in_, **kw))
        op.idx = len(self.ops[q])
        k = self.dma_rr[q]
        self.dma_rr[q] = (k + 1) % N_DMA_SEMS[q]
        self.dma_cnt[q][k] += 1
        op.dma_sem = (q, k)
        op.dma_val = 16 * self.dma_cnt[q][k]
        ev = ("d", q, k, op.dma_val)
        op.waits = self._deps(q, True, [in_], [out], ev)
        if self.dma_last[q][k] is not None:
            op.waits.append(self.dma_last[q][k])
        self.dma_last[q][k] = ev
        self.ops[q].append(op)
        return op

    def mm(self, out, lhsT, rhs, start=True, stop=True, **kw):
        return self.add("pe", lambda e: e.matmul(out, lhsT, rhs, start=start, stop=stop, **kw),
                        [lhsT, rhs], [out])

    def tr(self, out, in_, ident):
        return self.add("pe", lambda e: e.transpose(out, in_, ident), [in_, ident], [out])

    def actf(self, out, in_, func, bias=0.0, scale=1.0, eng="act"):
        return self.add("act", lambda e: e.activation(out, in_, func, bias=bias, scale=scale),
                        [in_, bias, scale], [out])

    def tt(self, eng, out, in0, in1, op):
        return self.add(eng, lambda e: e.tensor_tensor(out, in0, in1, op), [in0, in1], [out])

    def ts(self, eng, out, in0, s1, s2, op0, op1=None):
        if op1 is None:
            return self.add(eng, lambda e: e.tensor_scalar(out, in0, s1, None, op0), [in0, s1], [out])
        return self.add(eng, lambda e: e.tensor_scalar(out, in0, s1, s2, op0, op1), [in0, s1, s2], [out])

    def stt(self, out, in0, scalar, in1, op0, op1):
        return self.add("dve", lambda e: e.scalar_tensor_tensor(out, in0, scalar, in1, op0, op1),
                        [in0, scalar, in1], [out])

    def copy(self, eng, out, in_):
        if eng == "act":
            return self.add("act", lambda e: e.copy(out, in_), [in_], [out])
        return self.add(eng, lambda e: e.tensor_copy(out, in_), [in_], [out])

    def memset(self, eng, out, val):
        return self.add(eng, lambda e: e.memset(out, val), [], [out])

    def emit(self):
        nc = self.nc
        for e in ENGS:
            for op in self.ops[e]:
                for ev in op.waits:
                    if ev[0] == "c":
                        self.ops[ev[1]][ev[2]].inc = True
        for e in ENGS:
            c = 0
            for op in self.ops[e]:
                if op.inc and op.dma_sem is None:
                    c += 1
                op.ms = c
        with ExitStack() as es:
            csem = {e: es.enter_context(nc.semaphore("cs_" + e)) for e in ENGS}
            dsem = {q: [es.enter_context(nc.semaphore("ds_%s%d" % (q, k))) for k in range(n)]
                    for q, n in N_DMA_SEMS.items()}
            fin = es.enter_context(nc.semaphore("fin"))
            block = es.enter_context(nc.Block())
            ops = self.ops

            def run(eng_name, e):
                waited = {}
                for op in ops[eng_name]:
                    for ev in op.waits:
                        if ev[0] == "c":
                            sem = csem[ev[1]]
                            val = ops[ev[1]][ev[2]].ms
                            key = ("c", ev[1])
                        else:
                            sem = dsem[ev[1]][ev[2]]
                            val = ev[3]
                            key = ("d", ev[1], ev[2])
                        if waited.get(key, 0) >= val:
                            continue
                        waited[key] = val
                        e.wait_ge(sem, val)
                    ins = op.fn(e)
                    if op.dma_sem is not None:
                        ins.then_inc(dsem[op.dma_sem[0]][op.dma_sem[1]], 16)
                    elif op.inc:
                        ins.then_inc(csem[eng_name], 1)
                if eng_name in N_DMA_SEMS:
                    for k in range(N_DMA_SEMS[eng_name]):
                        v = 16 * self.dma_cnt[eng_name][k]
                        if v and waited.get(("d", eng_name, k), 0) < v:
                            e.wait_ge(dsem[eng_name][k], v)

            @block.tensor
            def _(e):
                run("pe", e)

            @block.scalar
            def _(e):
                run("act", e)

            @block.vector
            def _(e):
                run("dve", e)

            @block.gpsimd
            def _(e):
                run("pool", e)

            @block.sync
            def _(e):
                run("sp", e)

from concourse.bass_utils import run_bass_kernel_spmd

D = 1024
T = 512
SEQ = 2048
NT = SEQ // T
L = 2
NS = 16
ALPHA = float((2 * L) ** 0.25)
EPS = 1e-5
DFF = 2816
NFF = 22
C_XR, C_YR, C_Q, C_K, C_V, C_XQ, C_G = 0, 1024, 2048, 3072, 3328, 3584, 4608
P_IN, P_BR, P_OUT, P_GU, P_DN, P_MKV, NPAN = 0, 15, 21, 23, 34, 40, 44
NB = 4
V_CW, V_CB, V_BA, V_BX, V_LAM, V_BG = 0, 4, 5, 6, 7, 8
DV_HBA, DV_HBX, DV_CC, DV_HC = 0, 1, 2, 3


def build(do_sample=True):
    nc = bass.Bass("TRN2", target_bir_lowering=False)
    P = Prog(nc)

    def din(name, shape):
        return nc.dram_tensor(name, list(shape), F32, kind="ExternalInput").ap()

    def dout(name, shape):
        return nc.dram_tensor(name, list(shape), F32, kind="ExternalOutput").ap()

    xp = din("xp", [2, SEQ, D]); xs = din("xs", [NS, D])
    st_conv = din("st_conv", [L, NS, 3, D]); st_h = din("st_h", [L, NS, D])
    cwk = din("cwk", [L, NS, 128, 256]); cwv = din("cwv", [L, NS, 128, 256])
    cmk = din("cmk", [L, NS, 256, D]); cmv = din("cmv", [L, NS, 256, D])
    memp = din("memp", [2, 256, D])
    w_mem_kv = din("w_mem_kv", [L, D, 2 * D]); w_in = din("w_in", [L, D, 7680])
    w_branch = din("w_branch", [L, 3, D, D]); w_out = din("w_out", [L, D, D])
    w_gate_up = din("w_gate_up", [L, D, 2 * DFF]); w_down = din("w_down", [L, DFF, D])
    lru_wa = din("lru_wa", [L, 16, 64, 64]); lru_wx = din("lru_wx", [L, 16, 64, 64])
    sinks = din("sinks", [L, 16]); lnp = din("lnp", [L, 4, D]); vecfm = din("vecfm", [128, 22, 8])
    ident = din("ident", [128, 128]); maskb = din("maskb", [128, 2, 512]); onehot = din("onehot", [128, NS, NS])

    y_p = dout("y_p", [2, SEQ, D]); y_s = dout("y_s", [NS, D])
    p_conv = dout("p_conv", [L, 2, 3, D]); p_h = dout("p_h", [L, 2, D])
    p_wk = dout("p_wk", [L, 2, 128, 256]); p_wv = dout("p_wv", [L, 2, 128, 256])
    p_mk = dout("p_mk", [L, 2, 256, D]); p_mv = dout("p_mv", [L, 2, 256, D])
    s_conv = dout("s_conv", [L, NS, 3, D]); s_h = dout("s_h", [L, NS, D])
    s_wk = dout("s_wk", [L, NS, 128, 256]); s_wv = dout("s_wv", [L, NS, 128, 256])

    WS = nc.dram_tensor("WS", [L, NPAN, 128, 8, 512], BF16).ap()

    sb = nc.alloc_sbuf_tensor
    ident_f = sb("ident_f", [128, 128], F32); ident_b = sb("ident_b", [128, 128], BF16)
    ones_b = sb("ones_b", [128, 128], BF16); maskb_sb = sb("maskb_sb", [128, 2, 512], BF16)
    vec_sb = sb("vec_sb", [128, 22, 8], F32); DV = sb("DV", [128, L, 4, 8], F32)
    dtmp = sb("dtmp", [128, 4, 8], F32)
    BD = sb("BD", [128, L, 2, 8, 128], BF16)
    esink = sb("esink", [128, L, 16], F32)
    lnv = sb("lnv", [128, 4, D], F32)
    halo = sb("halo", [128, L, 8, 4], BF16); hst = sb("hst", [128, L, 8], F32)
    kTp = sb("kTp", [128, L, 4, 128], BF16); Vp = sb("Vp", [128, L, 4, 66], BF16)
    mkT = sb("mkT", [128, L, 8, 256], BF16); mv = sb("mv", [128, L, 2, D], BF16)
    WR = sb("WR", [128, NB, 8, 512], BF16)
    XA = sb("XA", [128, 4, D], F32); XB = sb("XB", [128, 4, D], F32)
    xT = sb("xT", [128, 8, T], BF16)
    mrgF = sb("mrgF", [128, 8, T], F32)
    pcv = sb("pcv", [128, 3, 8], F32); osm = sb("osm", [128, 512], F32)
    bst = sb("bst", [128, 2, 6], F32); mvb = sb("mvb", [128, 2], F32); rs = sb("rs", [128, 2], F32)
    nhalf = sb("nhalf", [128, 2], F32)
    SB = sb("SB", [128, 16896], BF16); SF = sb("SF", [128, 8192], F32)
    PS = nc.alloc_psum_tensor("PS", [128, 8, 512], F32)

    def sbv(a, b, t=512):
        return SB[:, a:b].rearrange("p (c t) -> p c t", t=t)

    def sfv(a, b, t=512):
        return SF[:, a:b].rearrange("p (c t) -> p c t", t=t)

    brT = sbv(0, 4096)
    xrT = sbv(4096, 8224, 516); xcT = sbv(8224, 10272); ti = sbv(10272, 12320); sP = sbv(12320, 14368)
    xh = sbv(14368, 15392)
    qT = sbv(4096, 8192); kTz = sbv(8192, 10752, 640)
    Vx = SB[:, 10752:12072].rearrange("p (b g e) -> p b g e", g=4, e=66)
    PT = SB[:, 12072:14120].rearrange("p (s k t) -> p s k t", k=2, t=512)
    xqT = sbv(4096, 8192); PTm = SB[:, 8192:10240].rearrange("p (s k t) -> p s k t", k=2, t=512)
    gsig = sbv(15872, 16896)
    mrgB = sbv(4096, 8192)
    actT = sbv(4096, 15360)
    aF = sfv(0, 2048); accb = sfv(2048, 3072); trb = sfv(3072, 4096); gbuf = sfv(4096, 5120)
    hbuf = sfv(5120, 6144); sqb = sfv(6144, 7168)
    On = SF[:, 0:1024]; den = SF[:, 1024:1040]; rden = SF[:, 1040:1056]
    rdm = sfv(0, 1024)
    gtmp = sfv(7168, 8192)
    tln = sfv(0, 2048, 1024)
    tbig = sfv(0, 4096, 1024)

    st = {"bank": 0, "ring": 0}

    def bank(n=1):
        b = st["bank"]
        if b % n:
            b += n - b % n
        if b + n > 8:
            b = 0
        st["bank"] = (b + n) % 8
        return b

    def load_panel(l, pidx, nk=8):
        slot = st["ring"] % NB
        st["ring"] += 1
        P.dma("sp", WR[:, slot, 0:nk, :], WS[l, pidx, :, 0:nk, :])
        return WR[:, slot]

    def fm_mm(b, pan, sub, rhsT, n=512, col0=0):
        for k in range(8):
            P.mm(PS[:, b, 0:n], pan[:, k, col0 + sub * 128: col0 + (sub + 1) * 128], rhsT[:, k, :],
                 start=(k == 0), stop=(k == 7))

    def vfm(l, i, c):
        return vec_sb[:, l * 11 + i, c:c + 1]

    def dv(l, i, c):
        return DV[:, l, i, c:c + 1]

    def conv_w(l, pidx, src, nk=8):
        P.dma("pool", WS[l, pidx, :, 0:nk, :], src.rearrange("(k p) n -> p k n", p=128))

    def conv_layer(l):
        for i in range(4):
            conv_w(l, P_MKV + i, w_mem_kv[l, :, i * 512:(i + 1) * 512])

        def win(i):
            if i in (4, 5):
                gh = i - 4
                for k in range(8):
                    for h in range(2):
                        P.dma("pool", WS[l, P_IN + i, :, k, :].rearrange("p (j h d) -> p j h d", j=4, h=2)[:, :, h, :],
                              w_in[l, k * 128:(k + 1) * 128, C_Q + gh * 512: C_Q + (gh + 1) * 512]
                              .rearrange("p (h j d) -> p j h d", h=2, j=4)[:, :, h, :])
            else:
                conv_w(l, P_IN + i, w_in[l, :, i * 512:(i + 1) * 512])

        def wbr(n):
            for h in range(2):
                conv_w(l, P_BR + 2 * n + h, w_branch[l, n, :, h * 512:(h + 1) * 512])
        for i in (0, 2, 1, 3):
            win(i)
        wbr(0); win(9); win(10)
        win(4); win(5); win(6)
        wbr(1); win(11); win(12)
        win(7); win(8)
        wbr(2); win(13); win(14)
        for h in range(2):
            conv_w(l, P_OUT + h, w_out[l, :, h * 512:(h + 1) * 512])
        for p in range(11):
            P.dma("pool", WS[l, P_GU + p, :, :, 0:256],
                  w_gate_up[l, :, p * 256:(p + 1) * 256].rearrange("(k p) n -> p k n", p=128))
            P.dma("pool", WS[l, P_GU + p, :, :, 256:512],
                  w_gate_up[l, :, DFF + p * 256: DFF + (p + 1) * 256].rearrange("(k p) n -> p k n", p=128))
        for half in range(2):
            for kg in range(3):
                nk = 8 if kg < 2 else 6
                conv_w(l, P_DN + kg * 2 + half, w_down[l, kg * 1024: kg * 1024 + nk * 128, half * 512:(half + 1) * 512], nk)

    def setup():
        P.dma("sp", ident_f[:], ident)
        P.dma("sp", vec_sb[:], vecfm)
        P.dma("pool", ident_b[:], ident)
        P.dma("pool", maskb_sb[:], maskb)
        P.memset("dve", ones_b[:], 1.0)
        P.memset("dve", nhalf[:], -0.5)
        P.memset("dve", BD[:], 0.0)
        for l in range(L):
            for gi, wsrc in enumerate((lru_wa, lru_wx)):
                for b in range(2):
                    P.dma("pool", BD[b * 64:(b + 1) * 64, l, gi, :, b * 64:(b + 1) * 64],
                          wsrc[l].rearrange("(c b) i j -> b i c j", b=2)[b])
            P.dma("sp", esink[:, l, :], sinks[l:l + 1, :].partition_broadcast(128))
            lam = vec_sb[:, l * 11 + V_LAM, :]
            P.ts("dve", dtmp[:, 2, :], lam, -1.0, None, ALU.mult)
            P.tt("dve", dtmp[:, 0, :], lam, dtmp[:, 2, :], ALU.max)
            P.actf(dtmp[:, 1, :], dtmp[:, 0, :], AF.Exp, scale=-1.0)
            P.actf(dtmp[:, 1, :], dtmp[:, 1, :], AF.Ln, bias=1.0)
            P.ts("dve", dtmp[:, 2, :], dtmp[:, 2, :], 0.0, None, ALU.max)
            P.tt("dve", dtmp[:, 3, :], dtmp[:, 1, :], dtmp[:, 2, :], ALU.add)
            P.ts("dve", DV[:, l, DV_CC, :], dtmp[:, 3, :], -8.0, None, ALU.mult)
            P.ts("dve", DV[:, l, DV_HC, :], dtmp[:, 3, :], -4.0, None, ALU.mult)
            P.ts("dve", DV[:, l, DV_HBA, :], vec_sb[:, l * 11 + V_BA, :], 0.5, None, ALU.mult)
            P.ts("dve", DV[:, l, DV_HBX, :], vec_sb[:, l * 11 + V_BX, :], 0.5, None, ALU.mult)
        P.actf(esink[:], esink[:], AF.Exp)

    def load_lnv(l):
        for i in range(4):
            P.dma("sp", lnv[:, i, :], lnp[l, i:i + 1, :].partition_broadcast(128))

    def make_xT(src, nblk=4):
        for blk in range(nblk):
            b = bank(2)
            for c in range(8):
                P.tr(PS[:, b + c // 4, (c % 4) * 128:(c % 4 + 1) * 128], src[:, blk, c * 128:(c + 1) * 128], ident_f[:])
            P.copy("act", xT[:, :, blk * 128:(blk + 1) * 128],
                   PS[:, b:b + 2, :].rearrange("p k (c t) -> p (k c) t", t=128))

    def layer_norm_block(which, tb, out, pn=128):
        P.add("dve", lambda e: e.bn_stats(bst[0:pn, 0, :], tb[:, 0:512]), [tb[:, 0:512]], [bst[0:pn, 0, :]])
        P.add("dve", lambda e: e.bn_stats(bst[0:pn, 1, :], tb[:, 512:1024]), [tb[:, 512:1024]], [bst[0:pn, 1, :]])
        bflat = bst[0:pn].rearrange("p a b -> p (a b)")
        P.add("dve", lambda e: e.bn_aggr(mvb[0:pn, 0:2], bflat), [bflat], [mvb[0:pn, 0:2]])
        P.ts("pool", rs[0:pn, 0:1], mvb[0:pn, 1:2], EPS, None, ALU.add)
        P.tt("pool", rs[0:pn, 1:2], rs[0:pn, 0:1], nhalf[0:pn, 0:1], ALU.pow)
        P.ts("dve", tb, tb, mvb[0:pn, 0:1], rs[0:pn, 1:2], ALU.subtract, ALU.mult)
        P.tt("pool", tb, tb, lnv[0:pn, 2 * which, :], ALU.mult)
        P.tt("pool", out, tb, lnv[0:pn, 2 * which + 1, :], ALU.add)

    def mkv_phase(s):
        P.dma("sp", XB[:, 0:2, :], memp[s].rearrange("(b p) d -> p b d", p=128))
        import os
        ksub = int(os.environ.get("KSUB", "9"))
        if ksub < 1:
            return
        make_xT(XB, 2)
        if ksub < 2:
            return
        for l in range(L):
            pans = [load_panel(l, P_MKV + i) for i in range(4)]
            if ksub < 3:
                continue
            for c in range(8):
                b = bank()
                for k in range(8):
                    P.mm(PS[:, b, 0:256], pans[c // 4][:, k, (c % 4) * 128:(c % 4 + 1) * 128], xT[:, k, 0:256],
                         start=(k == 0), stop=(k == 7))
                P.copy("act", mkT[:, l, c, :], PS[:, b, 0:256])
            if ksub < 4:
                continue
            for which in range(2):
                for mb in range(2):
                    b = bank(2)
                    for k in range(8):
                        for half in range(2):
                            P.mm(PS[:, b + half, :], xT[:, k, mb * 128:(mb + 1) * 128], pans[which * 2 + half][:, k, :],
                                 start=(k == 0), stop=(k == 7))
                    stg = tln[:, mb, :]
                    for half in range(2):
                        P.copy("dve", stg[:, half * 512:(half + 1) * 512], PS[:, b + half, :])
                    P.dma("pool", (p_mk if which == 0 else p_mv)[l, s, mb * 128:(mb + 1) * 128, :], stg)
                    if which == 1:
                        P.copy("act", mv[:, l, mb, :], stg)

    def lru_phase(l, s, tile):
        first = (tile == 0); last = (tile == NT - 1)
        if first:
            P.memset("pool", xrT[:, :, 0:3], 0.0)
        else:
            P.copy("pool", xrT[:, :, 0:3], halo[:, l, :, 0:3])
        for batch in range(2):
            cs = list(range(batch * 4, batch * 4 + 4))
            pxr = load_panel(l, P_IN + batch)
            for c in cs:
                b = bank()
                fm_mm(b, pxr, c % 4, xT)
                P.copy("act", xrT[:, c, 3:515], PS[:, b, :])
                if last:
                    P.copy("dve", pcv[:, :, c], PS[:, b, 509:512])
                acc = accb[:, c % 2, :]
                P.ts("dve", acc, xrT[:, c, 0:512], vfm(l, V_CW + 0, c), vfm(l, V_CB, c), ALU.mult, ALU.add)
                P.stt(acc, xrT[:, c, 1:513], vfm(l, V_CW + 1, c), acc, ALU.mult, ALU.add)
                P.stt(acc, xrT[:, c, 2:514], vfm(l, V_CW + 2, c), acc, ALU.mult, ALU.add)
                P.stt(xcT[:, c % 4, :], xrT[:, c, 3:515], vfm(l, V_CW + 3, c), acc, ALU.mult, ALU.add)
                b1 = bank(); P.mm(PS[:, b1, :], BD[:, l, 0, c, :], xcT[:, c % 4, :])
                b2 = bank(); P.mm(PS[:, b2, :], BD[:, l, 1, c, :], xcT[:, c % 4, :])
                tr_ = trb[:, c % 2, :]
                P.actf(tr_, PS[:, b1, :], AF.Tanh, bias=dv(l, DV_HBA, c), scale=0.5)
                P.actf(ti[:, c % 4, :], PS[:, b2, :], AF.Tanh, bias=dv(l, DV_HBX, c), scale=0.5)
                P.actf(aF[:, c % 4, :], tr_, AF.Exp, bias=dv(l, DV_HC, c), scale=dv(l, DV_HC, c))
            for c in cs:
                w2 = sqb[:, c % 2, :]
                P.tt("pool", w2, aF[:, c % 4, :], aF[:, c % 4, :], ALU.mult)
                P.actf(sP[:, c % 4, :], w2, AF.Sqrt, bias=0.25, scale=-0.25)
            pyr = load_panel(l, P_IN + 2 + batch)
            for c in cs:
                gb = gbuf[:, c % 2, :]; hb = hbuf[:, c % 2, :]; sq = sqb[:, c % 2, :]; xhh = xh[:, c % 2, :]
                P.stt(gb, ti[:, c % 4, :], 1.0, xcT[:, c % 4, :], ALU.add, ALU.mult)
                P.tt("pool", gb, gb, sP[:, c % 4, :], ALU.mult)
                init = 0.0 if first else hst[:, l, c:c + 1]
                af = aF[:, c % 4, :]
                P.add("dve", lambda e, hb=hb, af=af, gb=gb, init=init: e.tensor_tensor_scan(hb, af, gb, init, ALU.mult, ALU.add),
                      [af, gb, init], [hb])
                P.copy("pool", hst[:, l, c:c + 1], hb[:, 511:512])
                b = bank()
                fm_mm(b, pyr, c % 4, xT)
                P.actf(sq, PS[:, b, :], AF.Square)
                P.actf(xhh, PS[:, b, :], AF.Identity, scale=0.5)
                P.ts("pool", sq, sq, 0.044715, 1.0, ALU.mult, ALU.add)
                P.tt("pool", sq, sq, xhh, ALU.mult)
                P.actf(sq, sq, AF.Tanh, scale=1.5957691216057308)
                P.tt("pool", hb, hb, xhh, ALU.mult)
                P.stt(brT[:, c, :], sq, 1.0, hb, ALU.add, ALU.mult)
        P.copy("pool", halo[:, l, :, 0:3], xrT[:, :, 512:515])
        if last:
            b = bank()
            P.tr(PS[0:8, b, 0:128], hst[:, l, :], ident_f[:])
            for j in range(3):
                P.tr(PS[0:8, b, (j + 1) * 128:(j + 2) * 128], pcv[:, j, :], ident_f[:])
            P.copy("dve", osm[0:8, :], PS[0:8, b, :])
            P.dma("pool", p_h[l, s].rearrange("(c p) -> c p", p=128), osm[0:8, 0:128])
            for j in range(3):
                P.dma("pool", p_conv[l, s, j].rearrange("(c p) -> c p", p=128), osm[0:8, (j + 1) * 128:(j + 2) * 128])

    def branch_merge(l, n, rT, ncols=512):
        for nh in range(2):
            pbr = load_panel(l, P_BR + n * 2 + nh)
            pg = load_panel(l, P_IN + 9 + 2 * n + nh)
            for sub in range(4):
                nch = nh * 4 + sub
                b1 = bank(); fm_mm(b1, pbr, sub, rT, ncols)
                b2 = bank(); fm_mm(b2, pg, sub, xT[:, :, 0:ncols], ncols)
                sg = gsig[:, nch % 2, 0:ncols]
                gt_ = gtmp[:, nch % 2, 0:ncols]
                P.actf(sg, PS[:, b2, 0:ncols], AF.Sigmoid, bias=vfm(l, V_BG + n, nch))
                if n == 0:
                    P.tt("dve", mrgF[:, nch, 0:ncols], sg, PS[:, b1, 0:ncols], ALU.mult)
                elif n == 1:
                    P.tt("dve", gt_, sg, PS[:, b1, 0:ncols], ALU.mult)
                    P.tt("pool", mrgF[:, nch, 0:ncols], mrgF[:, nch, 0:ncols], gt_, ALU.add)
                else:
                    P.tt("dve", gt_, sg, PS[:, b1, 0:ncols], ALU.mult)
                    P.tt("pool", mrgB[:, nch, 0:ncols], mrgF[:, nch, 0:ncols], gt_, ALU.add)

    def swa_phase(l, s, tile):
        first = (tile == 0); last = (tile == NT - 1)
        for c in range(8):
            if c % 4 == 0:
                pq = load_panel(l, P_IN + 4 + c // 4)
            b = bank(); fm_mm(b, pq, c % 4, xT)
            P.copy("act" if c % 2 == 0 else "dve", qT[:, c, :], PS[:, b, :])
        pkv = load_panel(l, P_IN + 6)
        P.memset("pool", kTz[:], 0.0)
        P.memset("pool", Vx[:, :, :, 64:65], 1.0)
        if not first:
            P.copy("pool", kTz[:, :, 0:128], kTp[:, l])
            P.copy("pool", Vx[:, 0], Vp[:, l])
        for gh in range(2):
            b = bank(); fm_mm(b, pkv, gh, xT)
            P.copy("act", kTz[0:64, 2 * gh, 128:640], PS[0:64, b, :])
            P.copy("dve", kTz[64:128, 2 * gh + 1, 128:640], PS[64:128, b, :])
        for blk in range(4):
            b = bank()
            for k in range(8):
                P.mm(PS[:, b, 0:256], xT[:, k, blk * 128:(blk + 1) * 128], pkv[:, k, 256:512], start=(k == 0), stop=(k == 7))
            P.copy("act", Vx[:, 1 + blk, :, 0:64], PS[:, b, 0:256].rearrange("p (g d) -> p g d", d=64))
            if last and blk == 3:
                P.copy("dve", osm[:, 0:256], PS[:, b, 0:256])
                P.dma("pool", p_wv[l, s], osm[:, 0:256])
        if last:
            b = bank()
            for k in range(8):
                P.mm(PS[:, b, 0:256], xT[:, k, 384:512], pkv[:, k, 0:256], start=(k == 0), stop=(k == 7))
            P.copy("dve", osm[:, 256:512], PS[:, b, 0:256])
            P.dma("pool", p_wk[l, s], osm[:, 256:512])

        def S_part(qb, g, slot):
            kbs = [1] if (first and qb == 0) else [0, 1]
            for kb in kbs:
                bnk = 4 + slot * 2 + kb
                kc = (qb + kb) * 128
                P.mm(PS[:, bnk, :], kTz[:, g, kc:kc + 128], qT[:, 4 * (g // 2):4 * (g // 2) + 4, qb * 128:(qb + 1) * 128],
                     start=True, stop=False)
                P.mm(PS[:, bnk, :], ident_b[:], maskb_sb[:, kb, :], start=False, stop=True)
                P.actf(PT[:, slot, kb, :], PS[:, bnk, :], AF.Exp, scale=0.125)
            return kbs

        def PV_part(qb, g, slot, kbs):
            for j in range(4):
                for kb in kbs:
                    P.mm(PS[:, g, j * 128:j * 128 + 65], PT[:, slot, kb, j * 128:(j + 1) * 128], Vx[:, qb + kb, g, 0:65],
                         start=(kb == kbs[0]), stop=(kb == kbs[-1]))

        def finish_qb(qb):
            Oall = PS[:, 0:4, :].rearrange("p g (j e) -> p (g j) e", e=128)
            P.tt("dve", den, Oall[:, :, 64], esink[:, l, :], ALU.add)
            P.add("dve", lambda e: e.reciprocal(rden, den), [den], [rden])
            P.tt("dve", On.rearrange("p (h d) -> p h d", d=64), Oall[:, :, 0:64],
                 rden.unsqueeze(2).to_broadcast([128, 16, 64]), ALU.mult)
            for c in range(8):
                P.tr(PS[:, 2 + c // 4, (c % 4) * 128:(c % 4 + 1) * 128], On[:, c * 128:(c + 1) * 128], ident_f[:])
            P.copy("act", brT[:, :, qb * 128:(qb + 1) * 128], PS[:, 2:4, :].rearrange("p k (c t) -> p (k c) t", t=128))

        units = [(qb, g) for qb in range(4) for g in range(4)]
        prev = None
        for i, (qb, g) in enumerate(units):
            slot = i % 2
            kbs = S_part(qb, g, slot)
            if prev is not None:
                PV_part(*prev)
                if prev[1] == 3:
                    finish_qb(prev[0])
            prev = (qb, g, slot, kbs)
        PV_part(*prev)
        finish_qb(3)
        P.copy("pool", kTp[:, l], kTz[:, :, 512:640])
        P.copy("pool", Vp[:, l], Vx[:, 4])

    def mem_phase(l):
        for c in range(8):
            if c % 4 == 0:
                pxq = load_panel(l, P_IN + 7 + c // 4)
            b = bank(); fm_mm(b, pxq, c % 4, xT)
            P.copy("act" if c % 2 == 0 else "dve", xqT[:, c, :], PS[:, b, :])
        for hh in range(4):
            slot = hh % 2
            for mb in range(2):
                b = bank()
                for dc in range(2):
                    P.mm(PS[:, b, :], mkT[:, l, 2 * hh + dc, mb * 128:(mb + 1) * 128], xqT[:, 2 * hh + dc, :],
                         start=(dc == 0), stop=(dc == 1))
                P.actf(PTm[:, slot, mb, :], PS[:, b, :], AF.Exp, scale=1.0 / 16.0)
            b = bank()
            for mb in range(2):
                P.mm(PS[:, b, :], ones_b[:], PTm[:, slot, mb, :], start=(mb == 0), stop=(mb == 1))
            rd = rdm[:, slot, :]
            pden = PS[:, b, :]
            P.add("dve", lambda e, rd=rd, pden=pden: e.reciprocal(rd, pden), [pden], [rd])
            for dc in range(2):
                b2 = bank()
                for mb in range(2):
                    P.mm(PS[:, b2, :], mv[:, l, mb, hh * 256 + dc * 128: hh * 256 + (dc + 1) * 128], PTm[:, slot, mb, :],
                         start=(mb == 0), stop=(mb == 1))
                P.tt("dve", brT[:, 2 * hh + dc, :], PS[:, b2, :], rd, ALU.mult)

    def wout_ln1(l):
        p0 = load_panel(l, P_OUT); p1 = load_panel(l, P_OUT + 1)
        for blk in range(4):
            b = bank(2)
            for k in range(8):
                for half, pp in ((0, p0), (1, p1)):
                    P.mm(PS[:, b + half, :], mrgB[:, k, blk * 128:(blk + 1) * 128], pp[:, k, :], start=(k == 0), stop=(k == 7))
            tb = tln[:, blk % 2, :]
            P.stt(tb, XA[:, blk, :], ALPHA, PS[:, b:b + 2, :].rearrange("p k t -> p (k t)"), ALU.mult, ALU.add)
            layer_norm_block(0, tb, XB[:, blk, :])
        make_xT(XB)

    def ffn_ln2(l, dst):
        for p in range(11):
            pan = load_panel(l, P_GU + p)
            for jj in range(2):
                j = 2 * p + jj
                b1 = bank(); fm_mm(b1, pan, jj, xT)
                b2 = bank(); fm_mm(b2, pan, jj, xT, col0=256)
                sl = gsig[:, j % 2, :]
                P.actf(sl, PS[:, b1, :], AF.Silu)
                P.tt("dve", actT[:, j, :], sl, PS[:, b2, :], ALU.mult)
        for half in range(2):
            bb = 4 * half
            st["bank"] = (bb + 4) % 8
            for kg in range(3):
                nk = 8 if kg < 2 else 6
                pan = load_panel(l, P_DN + kg * 2 + half, nk)
                for blk in range(4):
                    for kk in range(nk):
                        k = kg * 8 + kk
                        P.mm(PS[:, bb + blk, :], actT[:, k, blk * 128:(blk + 1) * 128], pan[:, kk, :], start=(k == 0), stop=(k == NFF - 1))
            for blk in range(4):
                P.stt(tbig[:, blk, half * 512:(half + 1) * 512], XB[:, blk, half * 512:(half + 1) * 512], ALPHA,
                      PS[:, bb + blk, :], ALU.mult, ALU.add)
        for blk in range(4):
            layer_norm_block(1, tbig[:, blk, :], dst[:, blk, :])

    def prompt_step(s, tile, l):
        tok0 = tile * T
        if l == 0:
            P.dma("sp", XA[:], xp[s, tok0:tok0 + T, :].rearrange("(b p) d -> p b d", p=128))
        load_lnv(l)
        make_xT(XA)
        lru_phase(l, s, tile)
        branch_merge(l, 0, brT)
        swa_phase(l, s, tile)
        branch_merge(l, 1, brT)
        mem_phase(l)
        branch_merge(l, 2, brT)
        wout_ln1(l)
        if l == L - 1:
            ffn_ln2(l, XB)
            P.dma("pool", y_p[s, tok0:tok0 + T, :].rearrange("(b p) d -> p b d", p=128), XB[:])
        else:
            ffn_ln2(l, XA)

    onehot_sb = sb("onehot_sb", [128, NS, NS], BF16)

    def xT_small(src16):
        b = bank()
        for c in range(8):
            P.tr(PS[:, b, c * NS:(c + 1) * NS], src16[:, c * 128:(c + 1) * 128], ident_f[0:NS, 0:NS])
        P.copy("act", xT[:, :, 0:NS], PS[:, b, 0:8 * NS].rearrange("p (c t) -> p c t", t=NS))

    def T_back(dst_dram, srcT):
        b = bank(2)
        for c in range(8):
            P.tr(PS[0:NS, b + c // 4, (c % 4) * 128:(c % 4 + 1) * 128], srcT[:, c, :], ident_f[:])
        for half in range(2):
            P.copy("dve", osm[0:NS, :], PS[0:NS, b + half, :])
            P.dma("pool", dst_dram[:, half * 512:(half + 1) * 512], osm[0:NS, :])

    def ln16(which, tb, out):
        layer_norm_block(which, tb, out, 16)

    def sample_layer(l, XS, H1):
        xsT = xT[:, :, 0:NS]
        kdbg = int(os.environ.get("KDBG", "0"))
        ks2 = int(os.environ.get("KS2", "0"))
        ks3 = int(os.environ.get("KS3", "0"))
        stg = SF[:, 0:1024]
        P.dma("sp", stg[0:48, :], st_conv[l].rearrange("b j d -> (b j) d"))
        b = bank()
        for c in range(8):
            P.tr(PS[:, b, c * 48:(c + 1) * 48], stg[0:48, c * 128:(c + 1) * 128], ident_f[0:48, 0:48])
        scT = SF[:, 1024:1408].rearrange("p (c b j) -> p c b j", b=NS, j=3)
        P.copy("dve", SF[:, 1024:1408], PS[:, b, 0:384])
        if ks2 == 1:
            return
        P.dma("pool", s_conv[l, :, 0:2, :], st_conv[l, :, 1:3, :])
        stgh = SF[:, 1408:2432]
        P.dma("sp", stgh[0:NS, :], st_h[l])
        b = bank()
        for c in range(8):
            P.tr(PS[:, b, c * NS:(c + 1) * NS], stgh[0:NS, c * 128:(c + 1) * 128], ident_f[0:NS, 0:NS])
        H0 = SF[:, 2432:2560]
        P.copy("dve", H0, PS[:, b, 0:128])
        if ks2 == 2:
            return
        pxr0 = load_panel(l, P_IN + 0); pxr1 = load_panel(l, P_IN + 1)
        b = bank()
        for c in range(8):
            pan = pxr0 if c < 4 else pxr1
            for k in range(8):
                P.mm(PS[:, b, c * NS:(c + 1) * NS], pan[:, k, (c % 4) * 128:(c % 4 + 1) * 128], xsT[:, k, :], start=(k == 0), stop=(k == 7))
        XR = SF[:, 2560:2688]
        P.copy("dve", XR, PS[:, b, 0:128])
        if ks2 == 3:
            return
        b = bank(2)
        for k in range(8):
            for half, pan in ((0, pxr0), (1, pxr1)):
                P.mm(PS[0:NS, b + half, :], xsT[:, k, :], pan[:, k, :], start=(k == 0), stop=(k == 7))
        for half in range(2):
            P.copy("dve", osm[0:NS, :], PS[0:NS, b + half, :])
            P.dma("pool", s_conv[l, :, 2, half * 512:(half + 1) * 512], osm[0:NS, :])
        if ks2 == 4:
            return
        ACC = SF[:, 2688:2816]; XC = SB[:, 4096:4224]
        XR3 = XR.rearrange("p (c b) -> p c b", b=NS); ACC3 = ACC.rearrange("p (c b) -> p c b", b=NS)
        XC3 = XC.rearrange("p (c b) -> p c b", b=NS)
        for c in range(8):
            P.ts("dve", ACC3[:, c, :], scT[:, c, :, 0], vfm(l, V_CW + 0, c), vfm(l, V_CB, c), ALU.mult, ALU.add)
            P.stt(ACC3[:, c, :], scT[:, c, :, 1], vfm(l, V_CW + 1, c), ACC3[:, c, :], ALU.mult, ALU.add)
            P.stt(ACC3[:, c, :], scT[:, c, :, 2], vfm(l, V_CW + 2, c), ACC3[:, c, :], ALU.mult, ALU.add)
            P.stt(XC3[:, c, :], XR3[:, c, :], vfm(l, V_CW + 3, c), ACC3[:, c, :], ALU.mult, ALU.add)
        b1 = bank(); b2 = bank()
        for c in range(8):
            P.mm(PS[:, b1, c * NS:(c + 1) * NS], BD[:, l, 0, c, :], XC3[:, c, :])
            P.mm(PS[:, b2, c * NS:(c + 1) * NS], BD[:, l, 1, c, :], XC3[:, c, :])
        TR = SF[:, 2816:2944]; TI = SF[:, 2944:3072]; AA = SF[:, 3072:3200]; W2 = SF[:, 3200:3328]
        SPs = SF[:, 3328:3456]; G = SF[:, 3456:3584]; HN = SF[:, 3584:3712]; SQ = SF[:, 3712:3840]; XH = SF[:, 3840:3968]
        for c in range(8):
            sl = slice(c * NS, (c + 1) * NS)
            P.actf(TR[:, sl], PS[:, b1, sl], AF.Tanh, bias=dv(l, DV_HBA, c), scale=0.5)
            P.actf(TI[:, sl], PS[:, b2, sl], AF.Tanh, bias=dv(l, DV_HBX, c), scale=0.5)
            P.actf(AA[:, sl], TR[:, sl], AF.Exp, bias=dv(l, DV_HC, c), scale=dv(l, DV_HC, c))
        P.tt("pool", W2, AA, AA, ALU.mult)
        P.actf(SPs, W2, AF.Sqrt, bias=0.25, scale=-0.25)
        P.stt(G, TI, 1.0, XC, ALU.add, ALU.mult)
        P.tt("pool", G, G, SPs, ALU.mult)
        P.tt("dve", HN, AA, H0, ALU.mult)
        P.tt("dve", HN, HN, G, ALU.add)
        if ks2 == 6:
            return
        T_back(s_h[l], HN.rearrange("p (c b) -> p c b", b=NS))
        if ks2 == 7:
            return
        pyr0 = load_panel(l, P_IN + 2); pyr1 = load_panel(l, P_IN + 3)
        b = bank()
        for c in range(8):
            pan = pyr0 if c < 4 else pyr1
            for k in range(8):
                P.mm(PS[:, b, c * NS:(c + 1) * NS], pan[:, k, (c % 4) * 128:(c % 4 + 1) * 128], xsT[:, k, :], start=(k == 0), stop=(k == 7))
        P.actf(SQ, PS[:, b, 0:128], AF.Square)
        P.actf(XH, PS[:, b, 0:128], AF.Identity, scale=0.5)
        P.ts("pool", SQ, SQ, 0.044715, 1.0, ALU.mult, ALU.add)
        P.tt("pool", SQ, SQ, XH, ALU.mult)
        P.actf(SQ, SQ, AF.Tanh, scale=1.5957691216057308)
        P.tt("pool", G, HN, XH, ALU.mult)
        P.stt(brT[:, :, 0:NS], SQ.rearrange("p (c b) -> p c b", b=NS), 1.0, G.rearrange("p (c b) -> p c b", b=NS), ALU.add, ALU.mult)
        if kdbg == 1:
            return
        branch_merge(l, 0, brT[:, :, 0:NS], NS)
        if ks3 == 1:
            return

        pq0 = load_panel(l, P_IN + 4); pq1 = load_panel(l, P_IN + 5)
        b = bank()
        for c in range(8):
            pan = pq0 if c < 4 else pq1
            for k in range(8):
                P.mm(PS[:, b, c * NS:(c + 1) * NS], pan[:, k, (c % 4) * 128:(c % 4 + 1) * 128], xsT[:, k, :], start=(k == 0), stop=(k == 7))
        P.copy("act", qT[:, :, 0:NS], PS[:, b, 0:128].rearrange("p (c t) -> p c t", t=NS))
        if ks3 == 2:
            return
        pkv = load_panel(l, P_IN + 6)
        b = bank()
        for k in range(8):
            P.mm(PS[0:NS, b, :], xsT[:, k, :], pkv[:, k, :], start=(k == 0), stop=(k == 7))
        P.copy("dve", osm[0:NS, :], PS[0:NS, b, :])
        P.dma("pool", s_wk[l, :, 127, :], osm[0:NS, 0:256])
        P.dma("pool", s_wv[l, :, 127, :], osm[0:NS, 256:512])
        P.dma("pool", s_wk[l, :, 0:127, :], cwk[l, :, 1:128, :])
        P.dma("pool", s_wv[l, :, 0:127, :], cwv[l, :, 1:128, :])
        if ks3 == 3:
            return
        Kw = SF[:, 4096:8192].rearrange("p (b d) -> p b d", d=256)
        P.dma("sp", Kw, s_wk[l].rearrange("b w d -> w b d"))
        if ks3 == 4:
            return
        Vwx = SB[:, 8192:12416].rearrange("p (b g e) -> p b g e", g=4, e=66)
        P.memset("dve", Vwx[:, :, :, 64:65], 1.0)
        for g in range(4):
            P.dma("pool", Vwx[:, :, g, 0:64], s_wv[l, :, :, g * 64:(g + 1) * 64].rearrange("b w d -> w b d"))
        if ks3 == 5:
            return
        KwT = SB[:, 12416:16512].rearrange("p (b h w) -> p b h w", h=2, w=128)
        for b4 in range(4):
            b = bank(2)
            for i in range(8):
                bb_, gh = b4 * 4 + i // 2, i % 2
                P.tr(PS[:, b + i // 4, (i % 4) * 128:(i % 4 + 1) * 128], Kw[:, bb_, gh * 128:(gh + 1) * 128], ident_f[:])
            for half in range(2):
                P.copy("act" if half == 0 else "dve", KwT[:, b4 * 4 + half * 2: b4 * 4 + half * 2 + 2],
                       PS[:, b + half, :].rearrange("p (b h w) -> p b h w", h=2, w=128))
        bS0 = bank(2)
        for bb_ in range(NS):
            for g in range(4):
                r0 = (g % 2) * 64
                col = bb_ * 8 + (g // 2) * 4
                P.mm(PS[:, bS0 + g % 2, col:col + 4], KwT[r0:r0 + 64, bb_, g // 2, :],
                     qT[r0:r0 + 64, 4 * (g // 2):4 * (g // 2) + 4, bb_])
        Pw = SB[:, 16512:16768]
        Pw4 = Pw.rearrange("p (b a r j) -> p b a r j", a=2, r=2, j=4)
        for par in range(2):
            P.actf(Pw4[:, :, :, par, :], PS[:, bS0 + par, 0:128].rearrange("p (b a j) -> p b a j", a=2, j=4), AF.Exp, scale=0.125)
        if ks3 == 6:
            return
        Pz = SB[:, 4096:8192].rearrange("p (b h c) -> p b h c", h=16, c=NS)
        P.tt("dve", Pz, Pw.rearrange("p (b h) -> p b h", h=16).unsqueeze(3).to_broadcast([128, NS, 16, NS]),
             onehot_sb[:].unsqueeze(2).to_broadcast([128, NS, 16, NS]), ALU.mult)
        if ks3 == 7:
            return
        st["bank"] = 4
        for h in range(16):
            g = h // 4
            for bb_ in range(NS):
                P.mm(PS[0:NS, g, (h % 4) * 128:(h % 4) * 128 + 65], Pz[:, bb_, h, :], Vwx[:, bb_, g, 0:65],
                     start=(bb_ == 0), stop=(bb_ == NS - 1))
        Oall = PS[0:NS, 0:4, :].rearrange("p g (j e) -> p (g j) e", e=128)
        den16 = SF[0:NS, 3968:3984]; rden16 = SF[0:NS, 3984:4000]; On16 = SF[0:NS, 0:1024]
        P.tt("dve", den16, Oall[:, :, 64], esink[0:NS, l, :], ALU.add)
        P.add("dve", lambda e: e.reciprocal(rden16, den16), [den16], [rden16])
        P.tt("dve", On16.rearrange("p (h d) -> p h d", d=64), Oall[:, :, 0:64],
             rden16.unsqueeze(2).to_broadcast([NS, 16, 64]), ALU.mult)
        b = bank()
        for c in range(8):
            P.tr(PS[:, b, c * NS:(c + 1) * NS], On16[:, c * 128:(c + 1) * 128], ident_f[0:NS, 0:NS])
        P.copy("act", brT[:, :, 0:NS], PS[:, b, 0:128].rearrange("p (c t) -> p c t", t=NS))
        if kdbg == 2:
            return
        branch_merge(l, 1, brT[:, :, 0:NS], NS)

        pxq0 = load_panel(l, P_IN + 7); pxq1 = load_panel(l, P_IN + 8)
        b = bank()
        for c in range(8):
            pan = pxq0 if c < 4 else pxq1
            for k in range(8):
                P.mm(PS[:, b, c * NS:(c + 1) * NS], pan[:, k, (c % 4) * 128:(c % 4 + 1) * 128], xsT[:, k, :], start=(k == 0), stop=(k == 7))
        P.copy("act", xqT[:, :, 0:NS], PS[:, b, 0:128].rearrange("p (c t) -> p c t", t=NS))
        Kmb = SF[:, 4096:8192].rearrange("p (s m d) -> p s m d", s=2, m=2)
        Vmb = SB[:, 8192:12320].rearrange("p (s m h e) -> p s m h e", s=2, m=2, h=4)
        KmTb = SB[:, 12320:16416].rearrange("p (s c m) -> p s c m", s=2, m=256)
        Pmb = SB[:, 4224:4240].rearrange("p (s k) -> p s k", s=2)
        Pzm = SB[:, 4240:4496].rearrange("p (s k c) -> p s k c", s=2, c=NS)
        P.memset("dve", Vmb[:, :, :, :, 256:257], 1.0)
        for bb_ in range(NS):
            s_ = bb_ % 2
            P.dma("sp", Kmb[:, s_], cmk[l, bb_].rearrange("(m p) d -> p m d", p=128))
            for m_ in range(2):
                P.dma("pool", Vmb[:, s_, m_, :, 0:256], cmv[l, bb_, m_ * 128:(m_ + 1) * 128, :].rearrange("p (h d) -> p h d", d=256))
            for rnd in range(2):
                for half in range(2):
                    bk = half
                    for i in range(4):
                        c = half * 4 + i
                        P.tr(PS[:, bk, i * 128:(i + 1) * 128], Kmb[:, s_, rnd, c * 128:(c + 1) * 128], ident_f[:])
                    P.copy("act" if half == 0 else "dve", KmTb[:, s_, half * 4:half * 4 + 4, rnd * 128:(rnd + 1) * 128],
                           PS[:, bk, :].rearrange("p (c m) -> p c m", m=128))
            bS = 2 + s_
            for hh in range(4):
                for m_ in range(2):
                    for dc in range(2):
                        P.mm(PS[:, bS, m_ * 4 + hh: m_ * 4 + hh + 1], KmTb[:, s_, 2 * hh + dc, m_ * 128:(m_ + 1) * 128],
                             xqT[:, 2 * hh + dc, bb_:bb_ + 1], start=(dc == 0), stop=(dc == 1))
            P.actf(Pmb[:, s_, :], PS[:, bS, 0:8], AF.Exp, scale=1.0 / 16.0)
            P.tt("dve", Pzm[:, s_], Pmb[:, s_, :].unsqueeze(2).to_broadcast([128, 8, NS]),
                 onehot_sb[:, bb_:bb_ + 1, :].to_broadcast([128, 8, NS]), ALU.mult)
            for hh in range(4):
                for m_ in range(2):
                    P.mm(PS[0:NS, 4 + hh, 0:257], Pzm[:, s_, m_ * 4 + hh, :], Vmb[:, s_, m_, hh, 0:257],
                         start=(bb_ == 0 and m_ == 0), stop=(bb_ == NS - 1 and m_ == 1))
        Om = PS[0:NS, 4:8, :]
        dm = SF[0:NS, 3968:3972]; rdm16 = SF[0:NS, 3984:3988]
        P.copy("dve", dm, Om[:, :, 256])
        P.add("dve", lambda e: e.reciprocal(rdm16, dm), [dm], [rdm16])
        P.tt("dve", On16.rearrange("p (h d) -> p h d", d=256), Om[:, :, 0:256],
             rdm16.unsqueeze(2).to_broadcast([NS, 4, 256]), ALU.mult)
        st["bank"] = 0
        b = bank()
        for c in range(8):
            P.tr(PS[:, b, c * NS:(c + 1) * NS], On16[:, c * 128:(c + 1) * 128], ident_f[0:NS, 0:NS])
        P.copy("act", brT[:, :, 0:NS], PS[:, b, 0:128].rearrange("p (c t) -> p c t", t=NS))
        if kdbg == 3:
            return
        branch_merge(l, 2, brT[:, :, 0:NS], NS)
        if kdbg == 4:
            return

        p0 = load_panel(l, P_OUT); p1 = load_panel(l, P_OUT + 1)
        b = bank(2)
        for k in range(8):
            for half, pp in ((0, p0), (1, p1)):
                P.mm(PS[0:NS, b + half, :], mrgB[:, k, 0:NS], pp[:, k, :], start=(k == 0), stop=(k == 7))
        tb = tln[0:NS, 0, :]
        for half in range(2):
            P.stt(tb[:, half * 512:(half + 1) * 512], XS[:, half * 512:(half + 1) * 512], ALPHA, PS[0:NS, b + half, :], ALU.mult, ALU.add)
        ln16(0, tb, H1)
        if kdbg == 5:
            return
        xT_small(H1)
        for p in range(11):
            pan = load_panel(l, P_GU + p)
            for jj in range(2):
                j = 2 * p + jj
                b1 = bank(); fm_mm(b1, pan, jj, xsT, NS)
                b2 = bank(); fm_mm(b2, pan, jj, xsT, NS, col0=256)
                sl_ = gsig[:, j % 2, 0:NS]
                P.actf(sl_, PS[:, b1, 0:NS], AF.Silu)
                P.tt("dve", actT[:, j, 0:NS], sl_, PS[:, b2, 0:NS], ALU.mult)
        tb2 = tln[0:NS, 1, :]
        for half in range(2):
            bb = bank()
            for kg in range(3):
                nk = 8 if kg < 2 else 6
                pan = load_panel(l, P_DN + kg * 2 + half, nk)
                for kk in range(nk):
                    k = kg * 8 + kk
                    P.mm(PS[0:NS, bb, :], actT[:, k, 0:NS], pan[:, kk, :], start=(k == 0), stop=(k == NFF - 1))
            P.stt(tb2[:, half * 512:(half + 1) * 512], H1[:, half * 512:(half + 1) * 512], ALPHA, PS[0:NS, bb, :], ALU.mult, ALU.add)
        ln16(1, tb2, XS)

    def sample_phase():
        XS = XA[0:NS, 0, :]; H1 = XB[0:NS, 0, :]
        P.dma("pool", onehot_sb[:], onehot)
        P.dma("sp", XS, xs)
        for l in range(L):
            load_lnv(l)
            xT_small(XS)
            sample_layer(l, XS, H1)
            if int(os.environ.get("KDBG", "0")):
                break
        P.dma("pool", y_s, XS)

    import os
    stage = int(os.environ.get("KSTAGE", "99"))
    setup()
    for l in range(L):
        conv_layer(l)
    for s in range(2):
        if stage >= 1:
            mkv_phase(s)
        for tile in range(NT):
            for l in range(L):
                if stage >= 3 or (stage == 2 and s == 0 and tile == 0 and l == 0):
                    prompt_step(s, tile, l)
    if do_sample:
        sample_phase()
    P.emit()
    return nc


_NC_CACHE = {}


def kernel(x_prompt, x_sample, state_conv, state_h, cache_win_k, cache_win_v, cache_mem_k, cache_mem_v,
           mem_prompt, w_mem_kv, w_in, b_gates, conv_w, conv_b, lru_wa, lru_ba, lru_wx, lru_bx, lru_lambda,
           sinks, w_branch, w_out, ln1_g, ln1_b, w_gate_up, w_down, ln2_g, ln2_b):
    f = lambda a: np.ascontiguousarray(np.asarray(a, dtype=np.float32))
    import os
    ncores = int(os.environ.get("KCORES", "8"))
    nc = build()
    rows = []
    for l in range(L):
        rows += [conv_w[l, 0], conv_w[l, 1], conv_w[l, 2], conv_w[l, 3], conv_b[l], lru_ba[l], lru_bx[l],
                 lru_lambda[l], b_gates[l, 0], b_gates[l, 1], b_gates[l, 2]]
    vecfm = f(np.stack([np.asarray(r, np.float32) for r in rows]).reshape(22, 8, 128).transpose(2, 0, 1))
    lnp = f(np.stack([np.asarray(ln1_g), np.asarray(ln1_b), np.asarray(ln2_g), np.asarray(ln2_b)], axis=1))
    ident = np.eye(128, dtype=np.float32)
    m = np.arange(128)[:, None]; q = np.arange(128)[None, :]
    mb = np.zeros((128, 2, 4, 128), np.float32)
    mb[:, 1] = np.where(m <= q, 0.0, -30000.0)[:, None, :]
    mb[:, 0] = np.where(m > q, 0.0, -30000.0)[:, None, :]
    maskb = mb.reshape(128, 2, 512)
    onehot = np.ascontiguousarray(np.broadcast_to(np.eye(NS, dtype=np.float32)[None], (128, NS, NS)))
    shared = dict(w_mem_kv=f(w_mem_kv), w_in=f(w_in), w_branch=f(w_branch), w_out=f(w_out), w_gate_up=f(w_gate_up),
                  w_down=f(w_down), lru_wa=f(lru_wa), lru_wx=f(lru_wx), sinks=f(sinks), lnp=lnp, vecfm=vecfm,
                  ident=ident, maskb=maskb, onehot=onehot)
    xs_all = f(x_sample)[:, 0, :]
    in_maps = []
    for c in range(ncores):
        sl = slice(NS * c, NS * (c + 1))
        d = dict(shared)
        d.update(xp=f(x_prompt[2 * c:2 * c + 2]), xs=f(xs_all[sl]), st_conv=f(state_conv[:, sl]), st_h=f(state_h[:, sl]),
                 cwk=f(np.asarray(cache_win_k)[:, sl].reshape(L, NS, 128, 256)), cwv=f(np.asarray(cache_win_v)[:, sl].reshape(L, NS, 128, 256)),
                 cmk=f(np.asarray(cache_mem_k)[:, sl].reshape(L, NS, 256, D)), cmv=f(np.asarray(cache_mem_v)[:, sl].reshape(L, NS, 256, D)),
                 memp=f(mem_prompt[2 * c:2 * c + 2]))
        in_maps.append(d)
    res = run_bass_kernel_spmd(nc, in_maps, core_ids=list(range(ncores)))
    R = res.results
    cat = lambda k, ax: np.concatenate([np.asarray(r[k], np.float32) for r in R], axis=ax)
    y_prompt = cat("y_p", 0)
    y_sample = cat("y_s", 0).reshape(-1, 1, D)
    p_conv = cat("p_conv", 1); p_h = cat("p_h", 1)
    p_wk = cat("p_wk", 1).reshape(L, -1, 128, 4, 64); p_wv = cat("p_wv", 1).reshape(L, -1, 128, 4, 64)
    p_mk = cat("p_mk", 1).reshape(L, -1, 256, 4, 256); p_mv = cat("p_mv", 1).reshape(L, -1, 256, 4, 256)
    s_conv = cat("s_conv", 1); s_h = cat("s_h", 1)
    s_wk = cat("s_wk", 1).reshape(L, -1, 128, 4, 64); s_wv = cat("s_wv", 1).reshape(L, -1, 128, 4, 64)
    return (y_prompt, y_sample, p_conv, p_h, p_wk, p_wv, p_mk, p_mv, s_conv, s_h, s_wk, s_wv)
```
